# Optimizing a Trainium2 kernel written in Bass

```python
import math
import jax, jax.numpy as jnp
from jax import lax
import numpy as np

D_MODEL = 2048
BATCH = 2
SEQ = 8192
DEPTH = 1

SWA_Q_HEADS = 16
SWA_KV_HEADS = 2
SWA_HEAD_DIM = 64
SWA_WINDOW = 128
GDN_HEADS = 4
GDN_HEAD_DIM = 128
GDN_CONV = 4
GDN_CHUNK = 64
N_MEM = 256
XA_HEADS = 4
XA_HEAD_DIM = 128
D_FF = 4 * D_MODEL
N_BRANCH = 3
RMS_EPS = 1e-6
L2_EPS = 1e-6

SWA_Q_W = SWA_Q_HEADS * SWA_HEAD_DIM
SWA_KV_W = SWA_KV_HEADS * SWA_HEAD_DIM
GDN_W = GDN_HEADS * GDN_HEAD_DIM
XA_W = XA_HEADS * XA_HEAD_DIM
IN_SPLITS = (SWA_Q_W, SWA_KV_W, SWA_KV_W, 3 * GDN_W, GDN_HEADS, GDN_HEADS, GDN_W, XA_W, N_BRANCH * D_MODEL)
IN_WIDTH = sum(IN_SPLITS)

kernel_name = 'hybrid_swa_sink_gdn_memxattn_relu2_block'


def rms_norm(x, g):
    xf = x.astype(jnp.float32)
    y = xf * lax.rsqrt(jnp.mean(xf * xf, axis=-1, keepdims=True) + RMS_EPS)
    return (y * g.astype(jnp.float32)).astype(x.dtype)


def l2_norm(x):
    return x * lax.rsqrt(jnp.sum(x * x, axis=-1, keepdims=True) + L2_EPS)


def split_cols(t, sizes):
    idx, acc = [], 0
    for s in sizes[:-1]:
        acc += s
        idx.append(acc)
    return jnp.split(t, idx, axis=-1)


def sliding_window_attention(q, k, v, sinks):
    B, S, HQ, hd = q.shape
    HKV = k.shape[2]
    G = HQ // HKV
    W = SWA_WINDOW
    nb = S // W
    qb = q.reshape(B, nb, W, HKV, G, hd)
    kb = k.reshape(B, nb, W, HKV, hd)
    vb = v.reshape(B, nb, W, HKV, hd)

    def with_prev(t):
        prev = jnp.pad(t, ((0, 0), (1, 0), (0, 0), (0, 0), (0, 0)))[:, :-1]
        return jnp.concatenate([prev, t], axis=2)

    kc, vc = with_prev(kb), with_prev(vb)
    s = jnp.einsum('bnqhgd,bnkhd->bnhgqk', qb, kc).astype(jnp.float32) * (hd ** -0.5)
    qi = jnp.arange(W)[:, None]
    kj = jnp.arange(2 * W)[None, :]
    band = (kj > qi) & (kj <= qi + W)
    not_pad = (jnp.arange(nb)[:, None, None] > 0) | (kj >= W)[None]
    mask = band[None] & not_pad
    s = jnp.where(mask[None, :, None, None], s, -jnp.inf)
    sk = sinks.astype(jnp.float32).reshape(HKV, G)[None, None, :, :, None, None]
    m = jnp.maximum(jnp.max(s, axis=-1, keepdims=True), sk)
    p = jnp.exp(s - m)
    denom = jnp.sum(p, axis=-1, keepdims=True) + jnp.exp(sk - m)
    pr = (p / denom).astype(v.dtype)
    o = jnp.einsum('bnhgqk,bnkhd->bnqhgd', pr, vc)
    return o.reshape(B, S, HQ * hd)


def causal_depthwise_conv(x, w):
    K, C = w.shape
    return lax.conv_general_dilated(
        x, w.reshape(K, 1, C), window_strides=(1,), padding=[(K - 1, 0)],
        dimension_numbers=('NWC', 'WIO', 'NWC'), feature_group_count=C)


def chunked_gated_delta_rule(q, k, v, g, beta):
    B, S, H, dk = q.shape
    dv = v.shape[-1]
    C = GDN_CHUNK
    N = S // C

    def to_chunks(t):
        return t.reshape(B, N, C, H, -1).transpose(1, 0, 3, 2, 4)

    qc, kc, vc = to_chunks(q), to_chunks(k), to_chunks(v)
    gc = g.reshape(B, N, C, H).transpose(1, 0, 3, 2)
    bc = beta.reshape(B, N, C, H).transpose(1, 0, 3, 2)
    gcum = jnp.cumsum(gc, axis=-1)
    causal = jnp.tril(jnp.ones((C, C), dtype=bool))
    strict = jnp.tril(jnp.ones((C, C), dtype=bool), k=-1)
    decay = jnp.exp(jnp.where(causal, gcum[..., :, None] - gcum[..., None, :], -jnp.inf))
    kk = jnp.einsum('nbhcd,nbhed->nbhce', kc, kc)
    lower = jnp.where(strict, bc[..., :, None] * kk * decay, 0.0)
    a_mat = jnp.eye(C, dtype=q.dtype) + lower
    rhs = jnp.concatenate([vc * bc[..., None], kc * (bc * jnp.exp(gcum))[..., None]], axis=-1)
    sol = lax.linalg.triangular_solve(a_mat, rhs, left_side=True, lower=True, unit_diagonal=True)
    u, w = sol[..., :dv], sol[..., dv:]
    qk = jnp.einsum('nbhcd,nbhed->nbhce', qc, kc) * decay
    q_dec = qc * jnp.exp(gcum)[..., None]
    k_dec = kc * jnp.exp(gcum[..., -1:] - gcum)[..., None]
    g_last = jnp.exp(gcum[..., -1])

    def step(state, inp):
        qk_i, qd_i, kd_i, u_i, w_i, gl_i = inp
        v_new = u_i - jnp.einsum('bhcd,bhde->bhce', w_i, state)
        o = jnp.einsum('bhcd,bhde->bhce', qd_i, state) + jnp.einsum('bhce,bhef->bhcf', qk_i, v_new)
        state = state * gl_i[..., None, None] + jnp.einsum('bhcd,bhce->bhde', kd_i, v_new)
        return state, o

    state0 = jnp.zeros((B, H, dk, dv), dtype=q.dtype)
    _, o = lax.scan(step, state0, (qk, q_dec, k_dec, u, w, g_last))
    return o.transpose(1, 0, 3, 2, 4).reshape(B, S, H, dv)


def gated_deltanet(qkv, a, b, z, conv_w, a_log, dt_bias, norm_w):
    B, S, _ = qkv.shape
    H, dh = GDN_HEADS, GDN_HEAD_DIM
    f32 = jnp.float32
    qkv = jax.nn.silu(causal_depthwise_conv(qkv, conv_w))
    q, k, v = jnp.split(qkv, 3, axis=-1)
    q = l2_norm(q.reshape(B, S, H, dh).astype(f32)) * (dh ** -0.5)
    k = l2_norm(k.reshape(B, S, H, dh).astype(f32))
    v = v.reshape(B, S, H, dh).astype(f32)
    beta = jax.nn.sigmoid(b.astype(f32))
    g = -jnp.exp(a_log.astype(f32)) * jax.nn.softplus(a.astype(f32) + dt_bias.astype(f32))
    o = chunked_gated_delta_rule(q, k, v, g, beta)
    o = rms_norm(o, norm_w) * jax.nn.silu(z.reshape(B, S, H, dh).astype(f32))
    return o.reshape(B, S, H * dh).astype(qkv.dtype)


def memory_cross_attention(q, mkv):
    B, S, _ = q.shape
    q = q.reshape(B, S, XA_HEADS, XA_HEAD_DIM)
    mk, mv = jnp.split(mkv, 2, axis=-1)
    mk = mk.reshape(B, N_MEM, XA_HEADS, XA_HEAD_DIM)
    mv = mv.reshape(B, N_MEM, XA_HEADS, XA_HEAD_DIM)
    s = jnp.einsum('bshd,bmhd->bhsm', q, mk).astype(jnp.float32) * (XA_HEAD_DIM ** -0.5)
    p = jax.nn.softmax(s, axis=-1).astype(mv.dtype)
    return jnp.einsum('bhsm,bmhd->bshd', p, mv).reshape(B, S, XA_W)


def setup_inputs(seed: int = 0) -> dict:
    key = jax.random.key(seed)
    ks = jax.random.split(key, 20)
    f32 = jnp.float32
    L, D = DEPTH, D_MODEL

    def nrm(k, shape, scale):
        return jax.random.normal(k, shape, f32) * scale

    x = nrm(ks[0], (BATCH, SEQ, D), 1.0)
    mem = nrm(ks[1], (BATCH, N_MEM, D), 1.0)
    g_mix = 1.0 + nrm(ks[2], (L, D), 0.02)
    w_in = nrm(ks[3], (L, D, IN_WIDTH), D ** -0.5)
    sinks = nrm(ks[4], (L, SWA_Q_HEADS), 0.5)
    conv_w = nrm(ks[5], (L, GDN_CONV, 3 * GDN_W), GDN_CONV ** -0.5)
    a_log = jnp.log(jax.random.uniform(ks[6], (L, GDN_HEADS), f32, 1.0, 16.0))
    dt = jnp.exp(jax.random.uniform(ks[7], (L, GDN_HEADS), f32, math.log(1e-3), math.log(1e-1)))
    dt_bias = dt + jnp.log(-jnp.expm1(-dt))
    gdn_norm_w = 1.0 + nrm(ks[8], (L, GDN_HEAD_DIM), 0.02)
    g_mem = 1.0 + nrm(ks[9], (L, D), 0.02)
    w_mem_kv = nrm(ks[10], (L, D, 2 * XA_W), D ** -0.5)
    w_swa_up = nrm(ks[11], (L, SWA_Q_W, D), SWA_Q_W ** -0.5)
    w_gdn_up = nrm(ks[12], (L, GDN_W, D), GDN_W ** -0.5)
    w_xa_up = nrm(ks[13], (L, XA_W, D), XA_W ** -0.5)
    w_out = nrm(ks[14], (L, D, D), D ** -0.5)
    g_mlp = 1.0 + nrm(ks[15], (L, D), 0.02)
    w_mlp_in = nrm(ks[16], (L, D, D_FF), D ** -0.5)
    w_mlp_out = nrm(ks[17], (L, D_FF, D), D_FF ** -0.5)
    g_final = 1.0 + nrm(ks[18], (D,), 0.02)
    return {'x': x, 'mem': mem, 'g_mix': g_mix, 'w_in': w_in, 'sinks': sinks, 'conv_w': conv_w,
            'a_log': a_log, 'dt_bias': dt_bias, 'gdn_norm_w': gdn_norm_w, 'g_mem': g_mem,
            'w_mem_kv': w_mem_kv, 'w_swa_up': w_swa_up, 'w_gdn_up': w_gdn_up, 'w_xa_up': w_xa_up,
            'w_out': w_out, 'g_mlp': g_mlp, 'w_mlp_in': w_mlp_in, 'w_mlp_out': w_mlp_out,
            'g_final': g_final}


def reference(x, mem, g_mix, w_in, sinks, conv_w, a_log, dt_bias, gdn_norm_w, g_mem, w_mem_kv,
              w_swa_up, w_gdn_up, w_xa_up, w_out, g_mlp, w_mlp_in, w_mlp_out, g_final):
    B, S, D = x.shape
    h = x
    for l in range(DEPTH):
        n = rms_norm(h, g_mix[l])
        p = n @ w_in[l]
        q_a, k_a, v_a, qkv_b, a_b, b_b, z_b, q_c, gate_logits = split_cols(p, IN_SPLITS)
        y_a = sliding_window_attention(
            q_a.reshape(B, S, SWA_Q_HEADS, SWA_HEAD_DIM),
            k_a.reshape(B, S, SWA_KV_HEADS, SWA_HEAD_DIM),
            v_a.reshape(B, S, SWA_KV_HEADS, SWA_HEAD_DIM), sinks[l])
        y_b = gated_deltanet(qkv_b, a_b, b_b, z_b, conv_w[l], a_log[l], dt_bias[l], gdn_norm_w[l])
        mkv = rms_norm(mem, g_mem[l]) @ w_mem_kv[l]
        y_c = memory_cross_attention(q_c, mkv)
        g_a, g_b, g_c = jnp.split(jax.nn.sigmoid(gate_logits), N_BRANCH, axis=-1)
        merged = g_a * (y_a @ w_swa_up[l]) + g_b * (y_b @ w_gdn_up[l]) + g_c * (y_c @ w_xa_up[l])
        h = h + merged @ w_out[l]
        u = rms_norm(h, g_mlp[l]) @ w_mlp_in[l]
        h = h + jnp.square(jax.nn.relu(u)) @ w_mlp_out[l]
    return rms_norm(h, g_final)
```

```python
import contextlib
import numpy as np
import concourse.bass as bass
import concourse.mybir as mybir
from concourse.bass_utils import run_bass_kernel_spmd

F32 = mybir.dt.float32
BF16 = mybir.dt.bfloat16
AF = mybir.ActivationFunctionType
ALU = mybir.AluOpType
ESZ = {F32: 4, BF16: 2}
BIG = 30000.0
DBG = {}


class Sched:
    EPOCH = 4000
    RING = 16

    def __init__(self, nc):
        self.nc = nc
        self.engs = ["pe", "act", "dve", "pool", "sp"]
        self.ops = {e: [] for e in self.engs}
        self.cnt = {e: 0 for e in self.engs}
        self.epoch = {e: 0 for e in self.engs}
        self.seen = {e: {} for e in self.engs}
        self.track = {}
        self.ring_use = {}
        self.ring_next = {"sp": 0, "pool": 0}
        self.sem_keys = set()
        self.ninst = 0

    @staticmethod
    def fp(ap):
        t = ap.tensor
        name = t.name
        shape = list(t.shape)
        if type(t).__name__.startswith("DRam"):
            es = ESZ[ap.dtype]
            off = int(ap.offset)
            ext = 0
            for st, c in ap.ap:
                ext += (c - 1) * st
            return (name, 0, 1, off * es, (off + ext + 1) * es)
        row = 1
        for s in shape[1:]:
            row *= s
        off = int(ap.offset)
        pairs = list(ap.ap)
        p0 = off // row
        fo = off % row
        pstep, pcnt = pairs[0]
        assert pstep == row or pcnt == 1, (pairs, row)
        ext = 0
        for st, c in pairs[1:]:
            ext += (c - 1) * st
        es = ESZ[ap.dtype]
        lo, hi = fo * es, (fo + ext + 1) * es
        if type(t).__name__.startswith("PSum"):
            lo = lo // 2048 * 2048
            hi = (hi + 2047) // 2048 * 2048
        return (name, p0, p0 + pcnt, lo, hi)

    def _need(self, eng, waits, ent):
        sk, val = ent[4], ent[5]
        if sk[0] == "pe" and eng == "pe":
            return
        if self.seen[eng].get(sk, 0) >= val:
            return
        if waits.get(sk, 0) < val:
            waits[sk] = val

    def emit(self, eng, fn, reads=(), writes=(), dma=False, external_reads=()):
        waits = {}
        fr = [self.fp(a) for a in reads]
        fw = [self.fp(a) for a in writes]
        for (name, p0, p1, lo, hi) in fr:
            psum = name == "ps"
            for ent in self.track.get(name, ()):
                if (ent[6] or (psum and ent[4][0] != eng)) and ent[0] < p1 and p0 < ent[1] and ent[2] < hi and lo < ent[3]:
                    self._need(eng, waits, ent)
        for (name, p0, p1, lo, hi) in fw:
            for ent in self.track.get(name, ()):
                if ent[0] < p1 and p0 < ent[1] and ent[2] < hi and lo < ent[3]:
                    self._need(eng, waits, ent)
        if dma:
            slot = self.ring_next[eng]
            self.ring_next[eng] = (slot + 1) % self.RING
            uses = self.ring_use.get((eng, slot), 0)
            sk = ("dma", eng, slot)
            if uses > 0 and self.seen[eng].get(sk, 0) < 16 * uses:
                if waits.get(sk, 0) < 16 * uses:
                    waits[sk] = 16 * uses
            self.ring_use[(eng, slot)] = uses + 1
            val = 16 * (uses + 1)
            inc = 16
        else:
            if self.cnt[eng] >= self.EPOCH:
                self.epoch[eng] += 1
                self.cnt[eng] = 0
            self.cnt[eng] += 1
            sk = (eng, self.epoch[eng])
            val = self.cnt[eng]
            inc = 1
        self.sem_keys.add(sk)
        for wk, wv in waits.items():
            self.ops[eng].append(("w", wk, wv))
            self.seen[eng][wk] = wv
        self.ops[eng].append(("i", fn, sk, inc))
        self.ninst += 1
        for (name, p0, p1, lo, hi) in fw:
            lst = self.track.setdefault(name, [])
            lst[:] = [e for e in lst if not (p0 <= e[0] and e[1] <= p1 and lo <= e[2] and e[3] <= hi)]
            lst.append((p0, p1, lo, hi, sk, val, True))
        for (name, p0, p1, lo, hi) in fr:
            if name.startswith("__ro"):
                continue
            lst = self.track.setdefault(name, [])
            lst[:] = [e for e in lst if not ((not e[6]) and e[4] == sk and p0 <= e[0] and e[1] <= p1
                                            and lo <= e[2] and e[3] <= hi)]
            lst.append((p0, p1, lo, hi, sk, val, False))
        return sk, val

    def raw(self, eng, fn):
        self.ops[eng].append(("r", fn))

    def wait(self, eng, sk, val):
        self.ops[eng].append(("w", sk, val))

    def all_wait_all(self):
        finals = {}
        for name, lst in self.track.items():
            for e in lst:
                if finals.get(e[4], 0) < e[5]:
                    finals[e[4]] = e[5]
        return finals

    def mm(self, out, lhsT, rhs, start=True, stop=True):
        return self.emit("pe", lambda e: e.matmul(out, lhsT=lhsT, rhs=rhs, start=start, stop=stop),
                         reads=[lhsT, rhs], writes=[out])

    def tr(self, out, in_, ident):
        return self.emit("pe", lambda e: e.transpose(out=out, in_=in_, identity=ident),
                         reads=[in_, ident], writes=[out])

    def act(self, out, in_, func, bias=None, scale=None, accum_out=None):
        kw = {}
        rd = [in_]
        wr = [out]
        if bias is not None:
            kw["bias"] = bias
            if not isinstance(bias, (int, float)):
                rd.append(bias)
        if scale is not None:
            kw["scale"] = scale
            if not isinstance(scale, (int, float)):
                rd.append(scale)
        if accum_out is not None:
            kw["accum_out"] = accum_out
            wr.append(accum_out)
        return self.emit("act", lambda e: e.activation(out=out, in_=in_, func=func, **kw), reads=rd, writes=wr)

    def tt(self, eng, out, in0, in1, op):
        return self.emit(eng, lambda e: e.tensor_tensor(out=out, in0=in0, in1=in1, op=op),
                         reads=[in0, in1], writes=[out])

    def ts(self, eng, out, in0, s1, op0, s2=None, op1=None):
        rd = [in0]
        if not isinstance(s1, (int, float)):
            rd.append(s1)
        if s2 is not None and not isinstance(s2, (int, float)):
            rd.append(s2)
        if op1 is None:
            return self.emit(eng, lambda e: e.tensor_scalar(out=out, in0=in0, scalar1=s1, scalar2=None, op0=op0),
                             reads=rd, writes=[out])
        return self.emit(eng, lambda e: e.tensor_scalar(out=out, in0=in0, scalar1=s1, scalar2=s2, op0=op0, op1=op1),
                         reads=rd, writes=[out])

    def stt(self, eng, out, in0, scalar, in1, op0, op1):
        rd = [in0, in1]
        if not isinstance(scalar, (int, float)):
            rd.append(scalar)
        return self.emit(eng, lambda e: e.scalar_tensor_tensor(out=out, in0=in0, scalar=scalar, in1=in1,
                                                               op0=op0, op1=op1), reads=rd, writes=[out])

    def recip(self, out, in_):
        return self.emit("dve", lambda e: e.reciprocal(out=out, in_=in_), reads=[in_], writes=[out])

    def copy(self, eng, out, in_):
        if eng == "act":
            return self.act(out, in_, AF.Copy)
        return self.emit(eng, lambda e: e.tensor_copy(out=out, in_=in_), reads=[in_], writes=[out])

    def memset(self, eng, out, val):
        return self.emit(eng, lambda e: e.memset(out, val), reads=[], writes=[out])

    def dma(self, q, out, in_):
        rd = [] if type(in_.tensor).__name__.startswith("DRam") and in_.tensor.name.startswith("in_") else [in_]
        return self.emit(q, lambda e: e.dma_start(out=out, in_=in_), reads=rd, writes=[out], dma=True)

    def run(self, extra_sems=()):
        nc = self.nc
        with contextlib.ExitStack() as es:
            sems = {}
            for sk in sorted(self.sem_keys, key=str):
                sems[sk] = es.enter_context(nc.semaphore("s_" + "_".join(str(x) for x in sk)))
            for k in extra_sems:
                sems[k] = es.enter_context(nc.semaphore("x_" + str(k)))
            self.sems = sems
            block = es.enter_context(nc.Block())

            def runner(eng):
                def f(e):
                    for op in self.ops[eng]:
                        if op[0] == "w":
                            e.wait_ge(sems[op[1]], op[2])
                        elif op[0] == "i":
                            op[1](e).then_inc(sems[op[2]], op[3])
                        else:
                            op[1](e, sems)
                return f

            block.tensor(runner("pe"))
            block.scalar(runner("act"))
            block.vector(runner("dve"))
            block.gpsimd(runner("pool"))
            block.sync(runner("sp"))


class Arena:
    def __init__(self, tensor, nbytes):
        self.t = tensor
        self.n = nbytes
        self.off = 0
        self.peak = 0

    def alloc(self, shape, dtype):
        es = ESZ[dtype]
        n = 1
        for s in shape[1:]:
            n *= s
        nb = (n * es + 63) // 64 * 64
        lo = self.off
        self.off += nb
        self.peak = max(self.peak, self.off)
        assert self.off <= self.n, f"arena overflow {self.off} > {self.n}"
        ap = self.t[:, lo // 4:(lo + nb) // 4]
        if dtype != F32:
            ap = ap.bitcast(dtype)
        ap = ap[:, 0:n]
        if len(shape) == 3:
            ap = ap.rearrange("p (a b) -> p a b", b=shape[2])
        elif len(shape) == 4:
            ap = ap.rearrange("p (a b c) -> p a b c", b=shape[2], c=shape[3])
        if shape[0] != 128:
            ap = ap[0:shape[0]]
        return ap

    def mark(self):
        return self.off

    def release(self, m):
        self.off = m


RMS_EPS = 1e-6
L2_EPS = 1e-6
QA0, KA0, VA0, QKV0, AB0, Z0, QC0, G0 = 0, 1024, 1152, 1280, 2816, 2824, 3336, 3848
NCORES = 8


class Cfg:
    def __init__(self, D=2048, NT=16, BTL=4, GBL=2, DFF=8192, FFB=16, arena_kb=190):
        self.D, self.NT, self.BTL, self.GBL, self.DFF, self.FFB, self.arena_kb = D, NT, BTL, GBL, DFF, FFB, arena_kb


def build(cfg, stop=None):
    D, NT, BTL, GBL, DFF, FFB = cfg.D, cfg.NT, cfg.BTL, cfg.GBL, cfg.DFF, cfg.FFB
    KC = D // 128
    NB = NT // BTL
    BT = BTL * 128
    GT = GBL * 128
    FC = DFF // 128
    NTOK = NT * 128
    INW = G0 + 3 * D
    FBW = min(512, D)
    NFB = D // FBW
    assert FC % FFB == 0 and 128 + GT <= 512
    o_gmix, o_gmlp, o_gmem, o_conv = 0, KC, 2 * KC, 3 * KC
    o_sink = o_conv + 48
    o_alog = o_sink + 16
    o_dtb = o_alog + 4
    o_gnw = o_dtb + 4
    o_cm = o_gnw + 128
    o_gfin = o_cm + 8
    NV = o_gfin + D
    WTOT = (D * INW + D * 1024 + 2048 * D + D * D + 2 * D * DFF) // 128 + 128 * 256 * 2

    nc = bass.Bass("TRN2", target_bir_lowering=False)

    def din(name, shape):
        return nc.dram_tensor("in_" + name, shape, F32, kind="ExternalInput").ap()

    xh = din("x", [(NT + 1) * 128, D])
    memd = din("mem", [256, D])
    win = din("win", [D, INW])
    wmem = din("wmem", [D, 1024])
    wups = [din("wsup", [1024, D]), din("wgup", [512, D]), din("wxup", [512, D])]
    wout = din("wout", [D, D])
    w1 = din("w1", [D, DFF])
    w2 = din("w2", [DFF, D])
    cstd = din("cst", [128, 8 * 128])
    vecd = din("vec", [128, NV])
    y = nc.dram_tensor("y", [NTOK, D], F32, kind="ExternalOutput").ap()
    SCH = 131072
    NSC = (WTOT + SCH - 1) // SCH + 1
    scws = [nc.dram_tensor("scw%d" % i, [128, SCH], BF16).ap() for i in range(NSC)]
    scol = nc.dram_tensor("scol", [NTOK, 512], F32).ap()
    scqt = nc.dram_tensor("scqt", [128, NT * 512], BF16).ap()
    ccin_t = nc.dram_tensor("ccin", [512, 256], F32)
    ccout_t = nc.dram_tensor("ccout", [NCORES * 512, 256], F32)

    es = contextlib.ExitStack()
    AB = cfg.arena_kb * 1024
    art = es.enter_context(nc.sbuf_tensor("arena", [128, AB // 4], F32))
    pst = es.enter_context(nc.psum_tensor("ps", [128, 4096], F32))
    A = Arena(art, AB)
    S = Sched(nc)

    def bk(i, n=512, off=0):
        return pst[:, i * 512 + off:i * 512 + off + n]

    def bkb(i):
        return pst[:, i * 512:(i + 1) * 512].bitcast(BF16)

    def v3(ap, b):
        return ap.rearrange("p (a b) -> p a b", b=b)

    def flat(ap):
        return ap.rearrange("p a b -> p (a b)")

    cst = A.alloc([128, 8, 128], F32)
    S.dma("sp", flat(cst), cstd)
    vec = A.alloc([128, NV], F32)
    S.dma("sp", vec, vecd)
    identf, triu, bigL, bigU, bigQ = (cst[:, i, :] for i in range(5))
    identb = A.alloc([128, 128], BF16)
    S.copy("dve", identb, identf)
    maskb = A.alloc([128, 3, 128], BF16)
    S.copy("dve", maskb, cst[:, 5:8, :])
    onesf = A.alloc([128, 128], F32)
    S.memset("pool", onesf, 1.0)
    esink = A.alloc([128, 16], F32)
    S.act(esink, vec[:, o_sink:o_sink + 16], AF.Exp)
    negA = A.alloc([128, 4], F32)
    S.act(negA, vec[:, o_alog:o_alog + 4], AF.Exp)
    S.ts("dve", negA, negA, -1.0, ALU.mult)
    gfin = vec[:, o_gfin:o_gfin + D]
    gnw = vec[:, o_gnw:o_gnw + 128]
    wabp = A.alloc([128, KC, 8], BF16)
    S.dma("pool", wabp, win[:, AB0:AB0 + 8].rearrange("(a p) n -> p a n", p=128))

    xt = [A.alloc([128, D], F32) for _ in range(2)]
    xnb = [A.alloc([128, D], BF16) for _ in range(2)]
    nT = A.alloc([128, KC, 128 + BT], BF16)
    sm1 = A.alloc([128, 8], F32)
    NSLAB = 3
    SLABN = KC * 256 if D >= 512 else 16 * 256
    SLABN = max(SLABN, 16 * 256)
    slabs = [A.alloc([128, SLABN], BF16) for _ in range(NSLAB)]
    mkT = A.alloc([128, 4, 256], BF16)
    mvaug = A.alloc([128, 2, 4, 132], BF16)
    Sst = A.alloc([128, 4, 256], F32)
    Ssb = A.alloc([128, 4, 128], BF16)
    tick = [0]

    wc_off = {}
    wc_next = [0]
    slab_i = [0]

    def slab_load(key, pieces, nkc):
        buf = slabs[slab_i[0] % NSLAB]
        slab_i[0] += 1
        ncols = sum(int(p.shape[1]) for p in pieces)
        n = nkc * ncols
        assert n <= SLABN, (n, SLABN)
        fl = buf[:, 0:n]
        v = fl.rearrange("p (a b) -> p a b", b=ncols)
        if key not in wc_off:
            if wc_next[0] % SCH + n > SCH:
                wc_next[0] = (wc_next[0] // SCH + 1) * SCH
            wc_off[key] = wc_next[0]
            wc_next[0] += n
            assert wc_next[0] <= NSC * SCH
            c0 = 0
            for p in pieces:
                w = int(p.shape[1])
                S.dma("pool", v[:, :, c0:c0 + w], p.rearrange("(a p) n -> p a n", p=128))
                c0 += w
            o = wc_off[key]
            S.dma("sp", scws[o // SCH][:, o % SCH:o % SCH + n], fl)
        else:
            o = wc_off[key]
            S.dma("sp", fl, scws[o // SCH][:, o % SCH:o % SCH + n])
        return v

    def run_steps(steps, ahead=2):
        idxs = [i for i, s in enumerate(steps) if s[0] is not None]
        loaded = {}
        ptr = [0]

        def ensure(upto):
            while ptr[0] < len(idxs) and ptr[0] < upto:
                i = idxs[ptr[0]]
                loaded[i] = slab_load(*steps[i][0])
                ptr[0] += 1

        k = 0
        for i, (spec, fn) in enumerate(steps):
            if spec is not None:
                ensure(k + 1 + ahead)
                fn(loaded.pop(i))
                k += 1
            else:
                fn(None)

    def norm_A(src, src_is_dram=True):
        i = tick[0] % 2
        tick[0] += 1
        if src_is_dram:
            S.dma("sp", xt[i], src)
            xin = xt[i]
        else:
            xin = src
        ss = sm1[:, i * 2:i * 2 + 1]
        rs = sm1[:, i * 2 + 1:i * 2 + 2]
        S.act(xnb[i], xin, AF.Square, accum_out=ss)
        S.act(rs, ss, AF.Sqrt, scale=1.0 / D, bias=RMS_EPS)
        S.recip(rs, rs)
        S.act(xnb[i], xin, AF.Copy, scale=rs)
        return i

    def norm_B(i, dst_fn, gcol0):
        banks = [0, 1] if i == 0 else [2, 3]
        for kc in range(KC):
            b = banks[(kc // 8) % 2]
            S.tr(bkb(b)[:, (kc % 8) * 128:(kc % 8 + 1) * 128], xnb[i][:, kc * 128:(kc + 1) * 128], identb)
        for kc in range(KC):
            b = banks[(kc // 8) % 2]
            src_ps = bkb(b)[:, (kc % 8) * 128:(kc % 8 + 1) * 128]
            g = vec[:, gcol0 + kc:gcol0 + kc + 1]
            if (kc // 8) % 2 == 0:
                S.act(dst_fn(kc), src_ps, AF.Copy, scale=g)
            else:
                S.ts("dve", dst_fn(kc), src_ps, g, ALU.mult)

    def norm_many(items, gcol0, src_is_dram=True):
        prev = None
        for (src, dst_fn) in items:
            i = norm_A(src, src_is_dram)
            if prev is not None:
                norm_B(prev[0], prev[1], gcol0)
            prev = (i, dst_fn)
        norm_B(prev[0], prev[1], gcol0)

    def norm_to_T(src, dst_fn, gcol0, banks, src_is_dram=True, d=None):
        norm_many([(src, dst_fn)], gcol0, src_is_dram)

    def finish():
        fin = S.all_wait_all()
        for eng in S.engs:
            for k, v in fin.items():
                S.wait(eng, k, v)
        S.run(extra_sems=["cc"])
        es.close()
        print("instructions", S.ninst)
        return nc

    if stop == 'C0':
        return finish()
    m0 = A.mark()
    memT = A.alloc([128, KC, 256], BF16)
    norm_many([(memd[mt * 128:(mt + 1) * 128, :], (lambda kc, mt=mt: memT[:, kc, mt * 128:(mt + 1) * 128])) for mt in range(2)], o_gmem)
    if stop == 'M1':
        return finish()
    S.memset("pool", mvaug[:, :, :, 128:129], 1.0)
    if stop == 'M2':
        return finish()
    steps = []
    CW = 256

    def mk_step(s):
        def fn(sl):
            for c in range(2):
                h = s * 2 + c
                for kc in range(KC):
                    S.mm(bk(2 + c, 256), sl[:, kc, c * 128:(c + 1) * 128], memT[:, kc, :], start=(kc == 0), stop=(kc == KC - 1))
                S.copy("act", mkT[:, h, :], bk(2 + c, 256))
        return fn

    def mv_step(s):
        def fn(sl):
            for mt in range(2):
                for kc in range(KC):
                    S.mm(bk(4 + mt, 256), memT[:, kc, mt * 128:(mt + 1) * 128], sl[:, kc, :], start=(kc == 0), stop=(kc == KC - 1))
                S.copy("dve", mvaug[:, mt, s * 2:s * 2 + 2, 0:128], v3(bk(4 + mt, 256), 128))
        return fn

    for s in range(2):
        steps.append((("mk%d" % s, [wmem[:, s * CW:(s + 1) * CW]], KC), mk_step(s)))
    for s in range(2):
        steps.append((("mv%d" % s, [wmem[:, 512 + s * CW:512 + (s + 1) * CW]], KC), mv_step(s)))
    run_steps(steps)
    A.release(m0)
    if stop == 'M':
        return finish()

    g0 = A.mark()
    pc = A.alloc([128, 12, 3 + GT], F32)
    qkvs = A.alloc([128, 12, GT], F32)
    tmpc = [A.alloc([128, GT], F32) for _ in range(2)]
    sqb2 = [A.alloc([128, GT], F32) for _ in range(2)]
    rnb2 = [A.alloc([128, GT], F32) for _ in range(2)]
    abt = A.alloc([128, GBL, 8], F32)
    Wt = [A.alloc([128, 4, 128], F32) for _ in range(GBL)]
    kd = [A.alloc([128, 4, 128], BF16) for _ in range(GBL)]
    qd = [A.alloc([128, 4, 128], BF16) for _ in range(GBL)]
    qkTm = [A.alloc([128, 4, 128], BF16) for _ in range(GBL)]
    kTb = [A.alloc([128, 4, 128], BF16) for _ in range(GBL)]
    bva = [A.alloc([128, 4, 256], F32) for _ in range(GBL)]
    sm = [A.alloc([128, 16, 4], F32) for _ in range(GBL)]
    GA = A.alloc([128, 8, 128], F32)
    t3 = [A.alloc([128, 4, 128], F32) for _ in range(3)]
    LU = [A.alloc([128, 4, 128], F32) for _ in range(4)]
    Pb2 = [A.alloc([128, 4, 128], F32) for _ in range(2)]
    egb = A.alloc([128, 4, 128], F32)
    Sb = A.alloc([128, 4, 256], BF16)
    rr = A.alloc([128, 4, 256], F32)
    vnb = A.alloc([128, 4, 256], BF16)
    olt = [A.alloc([128, 512], F32) for _ in range(2)]
    qtt = [A.alloc([128, 512], BF16) for _ in range(2)]
    (G_, NLNB, BS, GC, GL, AL, NAL, CN, EGL, KDS, LNB, T0, T1) = range(13)

    for j in range(GBL):
        S.memset("pool", bva[j][:, :, 128:256], 0.0)
    S.memset("pool", Sst[:, :, 0:128], 0.0)
    for h in range(4):
        S.copy("pool", Sst[:, h, 128:256], identf)
    S.copy("pool", Sb, Sst)

    def gdn_prep(j):
        s = sm[j]
        tj = slice(j * 128, (j + 1) * 128)
        c = lambda k: s[:, k, :]
        S.tt("dve", c(T0), abt[:, j, 0:4], vec[:, o_dtb:o_dtb + 4], ALU.add)
        S.act(c(T0), c(T0), AF.Exp)
        S.act(c(T0), c(T0), AF.Ln, bias=1.0)
        S.tt("dve", c(G_), c(T0), negA, ALU.mult)
        S.act(c(T1), abt[:, j, 4:8], AF.Exp, scale=-1.0)
        S.act(c(NLNB), c(T1), AF.Ln, bias=1.0)
        S.ts("dve", c(T1), c(T1), 1.0, ALU.add)
        S.recip(c(BS), c(T1))
        S.mm(bk(0, 4), triu, c(G_))
        S.mm(bk(0, 4, 4), onesf, c(G_))
        S.copy("act", s[:, GC:GL + 1, :], v3(bk(0, 8), 4))
        S.tt("dve", c(AL), c(GC), c(NLNB), ALU.subtract)
        S.ts("dve", c(NAL), c(AL), -1.0, ALU.mult)
        S.ts("dve", c(LNB), c(NLNB), -1.0, ALU.mult)
        S.act(c(CN), c(AL), AF.Exp)
        S.ts("dve", c(CN), c(CN), -1.0, ALU.mult)
        S.act(c(EGL), c(GL), AF.Exp)
        S.tt("dve", c(KDS), c(GL), c(GC), ALU.subtract)
        S.act(c(KDS), c(KDS), AF.Exp)
        for h in range(4):
            S.ts("pool", GA[:, h, :], triu, s[:, G_, h:h + 1], ALU.mult)
            S.stt("dve", GA[:, 4 + h, :], identf, s[:, LNB, h:h + 1], GA[:, h, :], ALU.mult, ALU.add)
        S.mm(bk(1), onesf, flat(GA[:, 0:4, :]))
        S.mm(bk(2), onesf, flat(GA[:, 4:8, :]))
        for h in range(4):
            kTh = qkvs[:, 4 + h, tj]
            S.mm(bk(3, 128, h * 128), kTh, kTh)
            S.mm(bk(4, 128, h * 128), kTh, qkvs[:, h, tj])
        r1 = v3(bk(1), 128)
        r2 = v3(bk(2), 128)
        for h in range(4):
            S.stt("dve", t3[0][:, h, :], r1[:, h, :], s[:, NAL, h:h + 1], bigL, ALU.add, ALU.add)
            S.stt("dve", t3[1][:, h, :], r2[:, h, :], s[:, GC, h:h + 1], bigU, ALU.subtract, ALU.subtract)
            S.stt("dve", t3[2][:, h, :], r1[:, h, :], s[:, GC, h:h + 1], bigQ, ALU.subtract, ALU.subtract)
        S.act(t3[0], t3[0], AF.Exp, scale=-1.0)
        S.act(t3[1], t3[1], AF.Exp)
        S.act(t3[2], t3[2], AF.Exp)
        S.act(egb, r1, AF.Exp)
        L0, U0 = LU[0], LU[1]
        S.tt("dve", L0, v3(bk(3), 128), t3[0], ALU.mult)
        S.tt("dve", U0, v3(bk(3), 128), t3[1], ALU.mult)
        S.tt("dve", qkTm[j], v3(bk(4), 128), t3[2], ALU.mult)
        S.tt("pool", qd[j], qkvs[:, 0:4, tj], egb, ALU.mult)
        S.copy("pool", kTb[j], qkvs[:, 4:8, tj])
        S.tt("dve", Pb2[0], cst[:, 0:1, :].to_broadcast([128, 4, 128]), U0, ALU.subtract)
        for h in range(4):
            S.tr(bk(5, 128, h * 128), qkvs[:, 4 + h, tj], identf)
            S.tr(bk(6, 128, h * 128), qkvs[:, 8 + h, tj], identf)
        for h in range(4):
            S.act(kd[j][:, h, :], bk(5, 128, h * 128), AF.Copy, scale=s[:, KDS, h:h + 1])
            S.ts("dve", bva[j][:, h, 0:128], bk(6, 128, h * 128), s[:, BS, h:h + 1], ALU.mult)
    def gdn_neumann(j):
        Lc, Uc, Ln, Un = LU[0], LU[1], LU[2], LU[3]
        for h in range(4):
            S.mm(bk(5, 128, h * 128), Lc[:, h, :], Uc[:, h, :])
            S.mm(bk(6, 128, h * 128), Uc[:, h, :], Lc[:, h, :])
        S.copy("act", Un, v3(bk(5), 128))
        S.copy("act", Ln, v3(bk(6), 128))
        Lc, Uc, Ln, Un = Ln, Un, Lc, Uc
        yield
        pcur = 0
        for k in range(1, 7):
            for h in range(4):
                S.mm(bk(7, 128, h * 128), Lc[:, h, :], Pb2[pcur][:, h, :])
            if k < 6:
                for h in range(4):
                    S.mm(bk(5, 128, h * 128), Lc[:, h, :], Uc[:, h, :])
                    S.mm(bk(6, 128, h * 128), Uc[:, h, :], Lc[:, h, :])
            dst = Wt[j] if k == 6 else Pb2[1 - pcur]
            S.tt("dve", dst, v3(bk(7), 128), Pb2[pcur], ALU.add)
            pcur = 1 - pcur
            if k < 6:
                S.copy("act", Un, v3(bk(5), 128))
                S.copy("act", Ln, v3(bk(6), 128))
                Lc, Uc, Ln, Un = Ln, Un, Lc, Uc
            yield

    def gdn_scan(j, t):
        s = sm[j]
        kS = v3(pst[:, 0:1024], 256)
        vn = v3(pst[:, 1024:2048], 256)
        for h in range(4):
            S.mm(kS[:, h, :], kTb[j][:, h, :], Sb[:, h, :])
        for h in range(4):
            S.stt("dve", rr[:, h, :], kS[:, h, :], s[:, CN, h:h + 1], bva[j][:, h, :], ALU.mult, ALU.add)
        yield
        for h in range(4):
            S.mm(vn[:, h, :], Wt[j][:, h, :], rr[:, h, :])
        S.copy("act", vnb, vn)
        yield
        i = t % 2
        for h in range(4):
            S.mm(bk(4, 128, h * 128), qd[j][:, h, :], Sb[:, h, 0:128], start=True, stop=False)
            S.mm(bk(4, 128, h * 128), qkTm[j][:, h, :], vnb[:, h, 0:128], start=False, stop=True)
        S.copy("act", olt[i], bk(4))
        S.dma("sp", scol[t * 128:(t + 1) * 128, :], olt[i])
        yield
        for h in range(4):
            S.mm(bk(4, 128, h * 128), Sb[:, h, 128:256], qd[j][:, h, :], start=True, stop=False)
            S.mm(bk(4, 128, h * 128), vnb[:, h, 128:256], qkTm[j][:, h, :], start=False, stop=True)
        S.copy("act", qtt[i], bk(4))
        S.dma("sp", scqt[:, t * 512:(t + 1) * 512], qtt[i])
        yield
        for h in range(4):
            S.mm(kS[:, h, :], kd[j][:, h, :], vnb[:, h, :])
        for h in range(4):
            S.stt("dve", Sst[:, h, :], Sst[:, h, :], s[:, EGL, h:h + 1], kS[:, h, :], ALU.mult, ALU.add)
        S.copy("pool", Sb, Sst)
        yield

    NGB = NT // GBL
    pending = [None]
    for gb in range(NGB):
        first = gb == 0
        items = []
        if first:
            items.append((xh[0:128, :], (lambda kc: nT[:, kc, 0:128])))
        for j in range(GBL):
            t = gb * GBL + j
            items.append((xh[(t + 1) * 128:(t + 2) * 128, :], (lambda kc, j=j: nT[:, kc, 128 + j * 128:256 + j * 128])))
        norm_many(items, o_gmix)
        if not first:
            S.copy("pool", pc[:, :, 0:3], pc[:, :, GT:GT + 3])
        steps = []
        n0 = 0 if first else 128
        NN = 128 + GT - n0

        def qkv_step(sidx, n0=n0, NN=NN, first=first):
            def fn(sl):
                for c in range(2):
                    ch = sidx * 2 + c
                    b = 2 + (ch % 4)
                    for kc in range(KC):
                        S.mm(bk(b, NN), sl[:, kc, c * 128:(c + 1) * 128], nT[:, kc, n0:128 + GT], start=(kc == 0), stop=(kc == KC - 1))
                    if first:
                        S.copy("act", pc[:, ch, 0:3 + GT], bk(b, 3 + GT, 125))
                    else:
                        S.copy("act", pc[:, ch, 3:3 + GT], bk(b, GT))
            return fn

        for sidx in range(6):
            steps.append((("gq%d" % sidx, [win[:, QKV0 + sidx * 256:QKV0 + (sidx + 1) * 256]], KC), qkv_step(sidx)))

        def ab_step(_):
            for j in range(GBL):
                for kc in range(KC):
                    S.mm(bk(6, 8), nT[:, kc, 128 + j * 128:256 + j * 128], wabp[:, kc, :], start=(kc == 0), stop=(kc == KC - 1))
                S.copy("act", abt[:, j, :], bk(6, 8))
        steps.append((None, ab_step))

        def conv_step(_):
            for ch in range(12):
                eng = "dve" if ch % 2 == 0 else "dve"
                tb = tmpc[ch % 2]
                w = lambda i, ch=ch: vec[:, o_conv + ch * 4 + i:o_conv + ch * 4 + i + 1]
                S.ts(eng, tb, pc[:, ch, 3:3 + GT], w(3), ALU.mult)
                for i in (2, 1, 0):
                    S.stt("dve", tb, pc[:, ch, i:i + GT], w(i), tb, ALU.mult, ALU.add)
                S.act(qkvs[:, ch, :], tb, AF.Silu)
            for ch in range(8):
                sqb, rnb, lb = sqb2[ch % 2], rnb2[ch % 2], 6 + ch % 2
                S.tt("pool", sqb, qkvs[:, ch, :], qkvs[:, ch, :], ALU.mult)
                S.mm(bk(lb, GT), onesf, sqb)
                S.act(rnb, bk(lb, GT), AF.Sqrt, bias=L2_EPS)
                S.recip(rnb, rnb)
                if ch < 4:
                    S.stt("dve", qkvs[:, ch, :], qkvs[:, ch, :], 128.0 ** -0.5, rnb, ALU.mult, ALU.mult)
                else:
                    S.tt("dve", qkvs[:, ch, :], qkvs[:, ch, :], rnb, ALU.mult)
        steps.append((None, conv_step))

        def tiles_step(_, gb=gb):
            for j in range(GBL):
                gdn_prep(j)
                gens = [gdn_neumann(j)]
                if pending[0] is not None:
                    gens.append(pending[0])
                while gens:
                    for gnr in list(gens):
                        try:
                            next(gnr)
                        except StopIteration:
                            gens.remove(gnr)
                pending[0] = gdn_scan(j, gb * GBL + j)
        steps.append((None, tiles_step))
        run_steps(steps)
    for _ in pending[0]:
        pass

    if stop == 'G':
        return finish()
    ccin = ccin_t.ap()
    ccout = ccout_t.ap()
    sk, val = S.dma("sp", ccin.rearrange("(h p) c -> p h c", p=128), Sst)
    S.wait("pool", sk, val)
    S.raw("pool", lambda e, sems: e.collective_compute(
        "AllGather", ALU.bypass, replica_groups=[list(range(NCORES))],
        ins=[ccin_t.ap().opt()], outs=[ccout_t.ap().opt()]).then_inc(sems["cc"], 1))
    A.release(g0)
    if stop == 'CC':
        S.wait('sp', 'cc', 1)
        return finish()

    c0 = A.mark()
    allst = A.alloc([128, NCORES, 4, 256], F32)
    ttb = A.alloc([128, 4, 128], F32)
    newb = A.alloc([128, 4, 128], F32)
    S.wait("sp", "cc", 1)
    for r in range(NCORES):
        S.dma("sp", allst[:, r, :, :], ccout[r * 512:(r + 1) * 512, :].rearrange("(h p) c -> p h c", p=128))
    S.memset("pool", Sst[:, :, 0:128], 0.0)
    for r in range(NCORES):
        if r % 4 == 3:
            continue
        m = vec[:, o_cm + r:o_cm + r + 1]
        for h in range(4):
            S.tr(bk(0, 128, h * 128), allst[:, r, h, 128:256], identf)
        S.copy("act", ttb, v3(bk(0), 128))
        for h in range(4):
            S.mm(bk(1, 128, h * 128), ttb[:, h, :], Sst[:, h, 0:128])
        S.tt("dve", newb, v3(bk(1), 128), allst[:, r, :, 0:128], ALU.add)
        S.tt("dve", newb, newb, Sst[:, :, 0:128], ALU.subtract)
        S.stt("dve", Sst[:, :, 0:128], newb, m, Sst[:, :, 0:128], ALU.mult, ALU.add)
    S.copy("act", Ssb, Sst[:, :, 0:128])
    A.release(c0)
    if stop == 'CH':
        return finish()

    R1 = A.mark()
    qAT = A.alloc([128, 8, BT], BF16)
    kz = A.alloc([128, 2, 2, 128 + BT], BF16)
    vaug = A.alloc([128, BTL + 1, 2, 66], BF16)
    qCT = A.alloc([128, 4, BT], BF16)
    Pbuf = [A.alloc([128, 2, 8, 128], BF16) for _ in range(2)]
    ya = A.alloc([128, 1024], BF16)
    yc = A.alloc([128, BTL, 512], BF16)
    Pc = A.alloc([128, 2, BT], BF16)
    zs = A.alloc([128, BTL, 512], BF16)
    oltm = A.alloc([128, 512], F32)
    qttm = A.alloc([128, 512], BF16)
    of = A.alloc([128, 4, 128], F32)
    yb = A.alloc([128, 512], BF16)
    dn = A.alloc([128, 16], F32)
    r1end = A.mark()
    A.release(R1)
    hbuf = A.alloc([128, BTL, D], F32)
    A.off = max(A.off, r1end)
    mergedT = A.alloc([128, KC, BT], BF16)
    R3 = A.mark()
    yaT = A.alloc([128, 8, BT], BF16)
    ycT = A.alloc([128, 4, BT], BF16)
    ybT = A.alloc([128, 4, BT], BF16)
    r3end = A.mark()
    A.release(R3)
    uT = A.alloc([128, FFB, BT], BF16)
    A.off = max(A.off, r3end)
    gsb = [A.alloc([128, BT], F32) for _ in range(2)]
    macc = A.alloc([128, 2, BT], F32)
    tmpm = A.alloc([128, BT], F32)
    xst = [A.alloc([128, FBW], F32) for _ in range(2)]
    rl = [A.alloc([128, BT], F32) for _ in range(2)]
    print("arena peak KB", A.peak / 1024.0, "of", cfg.arena_kb)
    yTs = [yaT, ybT, ycT]
    nky = [8, 4, 4]


    for blk in range(NB):
        t0 = blk * BTL
        items = [(xh[t0 * 128:(t0 + 1) * 128, :], (lambda kc: nT[:, kc, 0:128]))]
        for j in range(BTL):
            items.append((xh[(t0 + j + 1) * 128:(t0 + j + 2) * 128, :], (lambda kc, j=j: nT[:, kc, 128 + j * 128:256 + j * 128])))
        norm_many(items, o_gmix)
        steps = []
        S.memset("pool", vaug[:, :, :, 64:65], 1.0)
        for g in range(2):
            S.memset("pool", kz[64:128, g, 0, :], 0.0)
            S.memset("pool", kz[0:64, g, 1, :], 0.0)

        def qa_step(sidx):
            def fn(sl):
                for c in range(2):
                    ch = sidx * 2 + c
                    b = 2 + ch % 4
                    for kc in range(KC):
                        S.mm(bk(b, BT), sl[:, kc, c * 128:(c + 1) * 128], nT[:, kc, 128:128 + BT], start=(kc == 0), stop=(kc == KC - 1))
                    S.copy("act" if ch % 2 == 0 else "dve", qAT[:, ch, :], bk(b, BT))
            return fn
        for sidx in range(4):
            steps.append((("qa%d" % sidx, [win[:, QA0 + sidx * 256:QA0 + (sidx + 1) * 256]], KC), qa_step(sidx)))

        def k_step(sl):
            for g in range(2):
                NKT = 128 + BT
                for n0 in range(0, NKT, 512):
                    nn = min(512, NKT - n0)
                    for kc in range(KC):
                        S.mm(bk(2 + g, nn), sl[:, kc, g * 128:(g + 1) * 128], nT[:, kc, n0:n0 + nn], start=(kc == 0), stop=(kc == KC - 1))
                    S.copy("act", kz[0:64, g, 0, n0:n0 + nn], pst[0:64, (2 + g) * 512:(2 + g) * 512 + nn])
                    S.copy("act", kz[64:128, g, 1, n0:n0 + nn], pst[64:128, (2 + g) * 512:(2 + g) * 512 + nn])

        def v_step(sl):
            for tt_ in range(BTL + 1):
                b = 4 + tt_ % 2
                for kc in range(KC):
                    S.mm(bk(b, 128), nT[:, kc, tt_ * 128:(tt_ + 1) * 128], sl[:, kc, :], start=(kc == 0), stop=(kc == KC - 1))
                S.copy("dve", vaug[:, tt_, :, 0:64], v3(bk(b, 128), 64))
        kp = [win[:, KA0:KA0 + 64], win[:, KA0:KA0 + 64], win[:, KA0 + 64:KA0 + 128], win[:, KA0 + 64:KA0 + 128]]
        steps.append((("kdup", kp, KC), k_step))
        steps.append((("va", [win[:, VA0:VA0 + 128]], KC), v_step))

        def qc_step(sidx):
            def fn(sl):
                for c in range(2):
                    ch = sidx * 2 + c
                    b = 2 + ch % 4
                    for kc in range(KC):
                        S.mm(bk(b, BT), sl[:, kc, c * 128:(c + 1) * 128], nT[:, kc, 128:128 + BT], start=(kc == 0), stop=(kc == KC - 1))
                    S.copy("act" if ch % 2 == 0 else "dve", qCT[:, ch, :], bk(b, BT))
            return fn
        for sidx in range(2):
            steps.append((("qc%d" % sidx, [win[:, QC0 + sidx * 256:QC0 + (sidx + 1) * 256]], KC), qc_step(sidx)))

        def z_step(sidx):
            def fn(sl):
                for j in range(BTL):
                    b = 2 + j % 4
                    for kc in range(KC):
                        S.mm(bk(b, 256), nT[:, kc, 128 + j * 128:256 + j * 128], sl[:, kc, :], start=(kc == 0), stop=(kc == KC - 1))
                    S.act(zs[:, j, sidx * 256:(sidx + 1) * 256], bk(b, 256), AF.Silu)
            return fn
        for sidx in range(2):
            steps.append((("z%d" % sidx, [win[:, Z0 + sidx * 256:Z0 + (sidx + 1) * 256]], KC), z_step(sidx)))

        def swa_step(_, blk=blk):
            lvl = DBG.get("swa", 99)
            for j in range(BTL):
                for g in range(2):
                    Pb = Pbuf[g]
                    for kb in range(2):
                        kcol = (j + kb) * 128
                        for hh in range(8):
                            h = g * 8 + hh
                            S.mm(bk(kb * 2 + hh // 4, 128, (hh % 4) * 128), kz[:, g, h % 2, kcol:kcol + 128],
                                 qAT[:, h // 2, j * 128:(j + 1) * 128])
                    if lvl < 2:
                        continue
                    for kb in range(2):
                        for hb in range(2):
                            S.act(flat(Pb[:, kb, hb * 4:(hb + 1) * 4, :]), bk(kb * 2 + hb), AF.Exp, scale=0.125)
                        if lvl < 3:
                            continue
                        mi = 0 if kb == 1 else (2 if (blk == 0 and j == 0) else 1)
                        S.tt("dve", Pb[:, kb, :, :], Pb[:, kb, :, :], maskb[:, mi:mi + 1, :].to_broadcast([128, 8, 128]), ALU.mult)
                    if lvl < 4:
                        continue
                    for hh in range(8):
                        ob = bk(4 + hh // 4, 65, (hh % 4) * 128)
                        for kb in range(2):
                            S.mm(ob, Pb[:, kb, hh, :], vaug[:, j + kb, g, 0:65], start=(kb == 0), stop=(kb == 1))
                    if lvl < 5:
                        continue
                    for half in range(2):
                        o3 = v3(bk(4 + half), 128)
                        h0 = g * 8 + half * 4
                        S.tt("dve", dn[:, 0:4], o3[:, :, 64], esink[:, h0:h0 + 4], ALU.add)
                        S.recip(dn[:, 4:8], dn[:, 0:4])
                        S.tt("dve", v3(ya[:, h0 * 64:(h0 + 4) * 64], 64), o3[:, :, 0:64],
                             dn[:, 4:8].unsqueeze(2).to_broadcast([128, 4, 64]), ALU.mult)
                if lvl < 6:
                    continue
                for c in range(8):
                    S.tr(bkb(6)[:, c * 128:(c + 1) * 128], ya[:, c * 128:(c + 1) * 128], identb)
                S.copy("act", yaT[:, :, j * 128:(j + 1) * 128], v3(bkb(6), 128))
        steps.append((None, swa_step))

        def xa_step(_):
            for h in range(4):
                for mt in range(2):
                    S.mm(bk(mt, BT), mkT[:, h, mt * 128:(mt + 1) * 128], qCT[:, h, :])
                    S.act(Pc[:, mt, :], bk(mt, BT), AF.Exp, scale=128.0 ** -0.5)
                for j in range(BTL):
                    ob = bk(2 + j % 2, 129)
                    for mt in range(2):
                        S.mm(ob, Pc[:, mt, j * 128:(j + 1) * 128], mvaug[:, mt, h, 0:129], start=(mt == 0), stop=(mt == 1))
                    S.recip(dn[:, 8:9], ob[:, 128:129])
                    S.ts("dve", yc[:, j, h * 128:(h + 1) * 128], ob[:, 0:128], dn[:, 8:9], ALU.mult)
            for j in range(BTL):
                for c in range(4):
                    S.tr(bkb(7)[:, c * 128:(c + 1) * 128], yc[:, j, c * 128:(c + 1) * 128], identb)
                S.copy("act", ycT[:, :, j * 128:(j + 1) * 128], v3(bkb(7)[:, 0:512], 128))
        steps.append((None, xa_step))

        def gfin_step(_, t0=t0):
            for j in range(BTL):
                t = t0 + j
                S.dma("sp", oltm, scol[t * 128:(t + 1) * 128, :])
                S.dma("sp", qttm, scqt[:, t * 512:(t + 1) * 512])
                for h in range(4):
                    S.mm(bk(3, 128, h * 128), qttm[:, h * 128:(h + 1) * 128], Ssb[:, h, :])
                S.tt("dve", flat(of), bk(3), oltm, ALU.add)
                for h in range(4):
                    S.act(yb[:, h * 128:(h + 1) * 128], of[:, h, :], AF.Square, accum_out=dn[:, 12 + h:13 + h])
                S.act(dn[:, 12:16], dn[:, 12:16], AF.Sqrt, scale=1.0 / 128, bias=RMS_EPS)
                S.recip(dn[:, 12:16], dn[:, 12:16])
                S.tt("dve", of, of, dn[:, 12:16].unsqueeze(2).to_broadcast([128, 4, 128]), ALU.mult)
                S.tt("dve", of, of, gnw.unsqueeze(1).to_broadcast([128, 4, 128]), ALU.mult)
                S.tt("dve", yb, flat(of), zs[:, j, :], ALU.mult)
                for c in range(4):
                    S.tr(bkb(7)[:, 512 + c * 128:512 + (c + 1) * 128], yb[:, c * 128:(c + 1) * 128], identb)
                S.copy("act", ybT[:, :, j * 128:(j + 1) * 128], v3(bkb(7)[:, 512:1024], 128))
        steps.append((None, gfin_step))

        def gate_step(i, fcg):
            def fn(sl):
                for c in range(2):
                    for kc in range(KC):
                        S.mm(bk(c, BT), sl[:, kc, c * 128:(c + 1) * 128], nT[:, kc, 128:128 + BT], start=(kc == 0), stop=(kc == KC - 1))
                    S.act(gsb[c], bk(c, BT), AF.Sigmoid)
            return fn

        def up_step(i, fcg):
            def fn(sl):
                for c in range(2):
                    b = 2 + (i % 2) * 2 + c
                    for kc in range(nky[i]):
                        S.mm(bk(b, BT), sl[:, kc, c * 128:(c + 1) * 128], yTs[i][:, kc, :], start=(kc == 0), stop=(kc == nky[i] - 1))
                    if i == 0:
                        S.tt("dve", macc[:, c, :], bk(b, BT), gsb[c], ALU.mult)
                    elif i == 1:
                        S.tt("dve", tmpm, bk(b, BT), gsb[c], ALU.mult)
                        S.tt("pool", macc[:, c, :], macc[:, c, :], tmpm, ALU.add)
                    else:
                        S.tt("dve", tmpm, bk(b, BT), gsb[c], ALU.mult)
                        S.tt("pool", mergedT[:, fcg * 2 + c, :], macc[:, c, :], tmpm, ALU.add)
            return fn
        for fcg in range(KC // 2):
            for i in range(3):
                gc0 = G0 + i * D + fcg * 256
                steps.append((("g%d_%d" % (i, fcg), [win[:, gc0:gc0 + 256]], KC), gate_step(i, fcg)))
                steps.append((("u%d_%d" % (i, fcg), [wups[i][:, fcg * 256:(fcg + 1) * 256]], nky[i]), up_step(i, fcg)))

        def out_step(fb, half, t0=t0):
            c0 = fb * FBW + half * 256
            def fn(sl):
                for j in range(BTL):
                    b = 4 + j % 4
                    for kc in range(KC):
                        S.mm(bk(b, 256), mergedT[:, kc, j * 128:(j + 1) * 128], sl[:, kc, :], start=(kc == 0), stop=(kc == KC - 1))
                    xs = xst[j % 2]
                    S.dma("sp", xs[:, 0:256], xh[(t0 + j + 1) * 128:(t0 + j + 2) * 128, c0:c0 + 256])
                    S.tt("dve", hbuf[:, j, c0:c0 + 256], bk(b, 256), xs[:, 0:256], ALU.add)
            return fn
        for fb in range(NFB):
            for half in range(FBW // 256):
                c0 = fb * FBW + half * 256
                steps.append((("wo%d_%d" % (fb, half), [wout[:, c0:c0 + 256]], KC), out_step(fb, half)))

        def n2_step(_):
            norm_many([(hbuf[:, j, :], (lambda kc, j=j: nT[:, kc, 128 + j * 128:256 + j * 128])) for j in range(BTL)], o_gmlp, src_is_dram=False)
        steps.append((None, n2_step))

        def w1_step(ffg, sidx):
            def fn(sl):
                for c in range(2):
                    ch = sidx * 2 + c
                    b = c
                    for kc in range(KC):
                        S.mm(bk(b, BT), sl[:, kc, c * 128:(c + 1) * 128], nT[:, kc, 128:128 + BT], start=(kc == 0), stop=(kc == KC - 1))
                    S.act(rl[c], bk(b, BT), AF.Relu)
                    S.tt("dve" if c == 0 else "pool", uT[:, ch, :], rl[c], rl[c], ALU.mult)
            return fn

        def w2_step(ffg, fb, half):
            c0 = fb * FBW + half * 256
            def fn(sl):
                for j in range(BTL):
                    b = 4 + j % 4
                    for kc in range(FFB):
                        S.mm(bk(b, 256), uT[:, kc, j * 128:(j + 1) * 128], sl[:, kc, :], start=(kc == 0), stop=(kc == FFB - 1))
                    S.tt("dve", hbuf[:, j, c0:c0 + 256], bk(b, 256), hbuf[:, j, c0:c0 + 256], ALU.add)
            return fn
        for ffg in range(FC // FFB):
            for sidx in range(FFB // 2):
                f0 = (ffg * FFB + sidx * 2) * 128
                steps.append((("w1_%d_%d" % (ffg, sidx), [w1[:, f0:f0 + 256]], KC), w1_step(ffg, sidx)))
            for fb in range(NFB):
                for half in range(FBW // 256):
                    c0 = fb * FBW + half * 256
                    steps.append((("w2_%d_%d_%d" % (ffg, fb, half),
                                   [w2[ffg * FFB * 128:(ffg + 1) * FFB * 128, c0:c0 + 256]], FFB), w2_step(ffg, fb, half)))

        def fin_step(_, t0=t0):
            for j in range(BTL):
                i = tick[0] % 2
                tick[0] += 1
                ss = sm1[:, 4 + i * 2:5 + i * 2]
                rs = sm1[:, 5 + i * 2:6 + i * 2]
                S.act(xnb[i], hbuf[:, j, :], AF.Square, accum_out=ss)
                S.act(rs, ss, AF.Sqrt, scale=1.0 / D, bias=RMS_EPS)
                S.recip(rs, rs)
                S.stt("dve", xt[i], hbuf[:, j, :], rs, gfin, ALU.mult, ALU.mult)
                S.dma("sp", y[(t0 + j) * 128:(t0 + j + 1) * 128, :], xt[i])
        steps.append((None, fin_step))
        if stop is not None and stop.startswith('S'):
            run_steps(steps[:int(stop[1:])])
            return finish()
        run_steps(steps)

    return finish()


def make_in_maps(inp, cfg):
    D, NT = cfg.D, cfg.NT
    KC = D // 128
    TPC = NT * 128
    f = lambda a: np.ascontiguousarray(np.asarray(a, dtype=np.float32))
    x = f(inp["x"])
    B, SEQ, _ = x.shape
    PPB = SEQ // TPC
    assert B * PPB == NCORES
    ii = np.arange(128)
    ident = (ii[:, None] == ii[None, :]).astype(np.float32)
    triu = (ii[:, None] <= ii[None, :]).astype(np.float32)
    bigL = np.where(ii[:, None] > ii[None, :], 0.0, BIG).astype(np.float32)
    bigU = np.where(ii[None, :] > ii[:, None], 0.0, BIG).astype(np.float32)
    bigQ = np.where(ii[None, :] >= ii[:, None], 0.0, BIG).astype(np.float32)
    mcur = (ii[:, None] <= ii[None, :]).astype(np.float32)
    mprev = (ii[:, None] > ii[None, :]).astype(np.float32)
    rep = lambda v: np.broadcast_to(f(v).reshape(1, -1), (128, f(v).size))
    shared = {
        "in_win": f(inp["w_in"][0]), "in_wmem": f(inp["w_mem_kv"][0]), "in_wsup": f(inp["w_swa_up"][0]),
        "in_wgup": f(inp["w_gdn_up"][0]), "in_wxup": f(inp["w_xa_up"][0]), "in_wout": f(inp["w_out"][0]),
        "in_w1": f(inp["w_mlp_in"][0]), "in_w2": f(inp["w_mlp_out"][0]),
    }
    colmaj = lambda v: f(v).reshape(KC, 128).T
    convw = f(inp["conv_w"][0]).reshape(4, 12, 128).transpose(2, 1, 0).reshape(128, 48)
    maps = []
    for c in range(NCORES):
        b, p = c // PPB, c % PPB
        xhalo = np.zeros((TPC + 128, D), np.float32)
        xhalo[128:] = x[b, p * TPC:(p + 1) * TPC]
        if p > 0:
            xhalo[:128] = x[b, p * TPC - 128:p * TPC]
        cst = np.concatenate([ident, triu, bigL, bigU, bigQ, mcur, mprev, mprev * (1.0 if p > 0 else 0.0)], axis=1)
        cm = np.zeros(8, np.float32)
        for r in range(NCORES):
            if r // PPB == b and r % PPB < p:
                cm[r] = 1.0
        vec = np.concatenate([
            colmaj(inp["g_mix"][0]), colmaj(inp["g_mlp"][0]), colmaj(inp["g_mem"][0]), convw,
            rep(inp["sinks"][0]), rep(inp["a_log"][0]), rep(inp["dt_bias"][0]), rep(inp["gdn_norm_w"][0]),
            rep(cm), rep(inp["g_final"])], axis=1)
        m = dict(shared)
        m["in_x"] = xhalo
        m["in_mem"] = f(inp["mem"][b])
        m["in_cst"] = np.ascontiguousarray(cst, dtype=np.float32)
        m["in_vec"] = np.ascontiguousarray(vec, dtype=np.float32)
        maps.append(m)
    return maps


def assemble(results, cfg, B, SEQ):
    TPC = cfg.NT * 128
    PPB = SEQ // TPC
    out = np.zeros((B, SEQ, cfg.D), np.float32)
    for c in range(NCORES):
        b, p = c // PPB, c % PPB
        out[b, p * TPC:(p + 1) * TPC] = results[c]["y"]
    return out


def kernel(**inputs):
    cfg = Cfg()
    nc = build(cfg)
    maps = make_in_maps(inputs, cfg)
    res = run_bass_kernel_spmd(nc, maps, core_ids=list(range(NCORES)))
    B, SEQ, _ = np.asarray(inputs["x"]).shape
    return assemble(res.results, cfg, B, SEQ)
```
